# Optimizing a Trainium2 kernel written in Bass

```python
import jax
import jax.numpy as jnp
from jax import lax
import numpy as np


D_MODEL = 1024
BATCH = 1
SEQ = 16384
DEPTH = 2
DEC_BATCH = 8
DEC_SEQ = 32
PAST_LEN = 2048

CHUNK = 64
MIX_BRANCH = D_MODEL // 2
POOL_WINDOWS = (2, 4, 8, 16)
POOL_GROUPS = 4
POOL_WIDTH = MIX_BRANCH
POOL_GW = POOL_WIDTH // POOL_GROUPS
POOL_HIST = 15
GLA_HEADS = 4
GLA_QK_W = D_MODEL // 4
GLA_V_W = MIX_BRANCH
GLA_DK = GLA_QK_W // GLA_HEADS
GLA_DV = GLA_V_W // GLA_HEADS
GLA_RANK = 16
GLA_TAU = 16.0
ATT_HEADS = 8
ATT_DH = 64
ATT_WIDTH = ATT_HEADS * ATT_DH
LEFT_CHUNKS = 8
ATT_WIN = LEFT_CHUNKS * CHUNK
MAX_REL = 128
N_BRANCH = 3
D_FF = -(-8 * D_MODEL // (3 * 256)) * 256
D_IN = POOL_WIDTH + 2 * GLA_QK_W + 2 * GLA_V_W + GLA_RANK + 3 * ATT_WIDTH + N_BRANCH * D_MODEL
EPS = 1e-6

kernel_name = "hybrid_pool_gla_chunkattn_stream_step"


def rms_norm(x, g):
    xf = x.astype(jnp.float32)
    y = xf * lax.rsqrt(jnp.mean(xf * xf, axis=-1, keepdims=True) + EPS)
    return (y * g.astype(jnp.float32)).astype(x.dtype)


def in_proj_offsets():
    sizes = (POOL_WIDTH, GLA_QK_W, GLA_QK_W, GLA_V_W, GLA_V_W, GLA_RANK,
             ATT_WIDTH, ATT_WIDTH, ATT_WIDTH, N_BRANCH * D_MODEL)
    return [int(o) for o in np.cumsum(sizes)[:-1]]


def pool_mixer(u, hist, pos0, pool_map, pool_scale):
    B, T, _ = u.shape
    f32 = jnp.float32
    full = jnp.concatenate([hist.astype(u.dtype), u], axis=1)
    ff = full.astype(f32)
    cs = jnp.concatenate([jnp.zeros((B, 1, POOL_WIDTH), f32), jnp.cumsum(ff, axis=1)], axis=1)
    pos = (pos0 + jnp.arange(T)).astype(f32)
    end = cs[:, POOL_HIST + 1:POOL_HIST + 1 + T]
    groups = []
    for g, w in enumerate(POOL_WINDOWS):
        sl = slice(g * POOL_GW, (g + 1) * POOL_GW)
        start = cs[:, POOL_HIST + 1 - w:POOL_HIST + 1 - w + T, sl]
        cnt = jnp.minimum(float(w), pos + 1.0)[None, :, None]
        groups.append((end[..., sl] - start) / cnt - ff[:, POOL_HIST:, sl])
    p = jnp.stack(groups, axis=2)
    y = jnp.einsum('btgc,gcd->btgd', p, pool_map.astype(f32)).reshape(B, T, POOL_WIDTH)
    y = y * pool_scale.astype(f32)
    return y.astype(u.dtype), full[:, -POOL_HIST:]


def gla_recurrence(q, k, v, log_a, s0):
    B, T, H, DK = q.shape
    DV = v.shape[-1]
    L = min(CHUNK, T)
    n = T // L

    def blocks(a):
        return a.reshape(B, n, L, H, a.shape[-1]).transpose(1, 0, 3, 2, 4)

    causal = jnp.tril(jnp.ones((L, L), dtype=bool))[:, :, None]

    def step(s, blk):
        qb, kb, vb, gb = blk
        b = jnp.cumsum(gb, axis=2)
        o_inter = jnp.einsum('bhld,bhde->bhle', qb * jnp.exp(b), s)
        diff = b[:, :, :, None, :] - b[:, :, None, :, :]
        decay = jnp.exp(jnp.where(causal, diff, -jnp.inf))
        att = jnp.einsum('bhid,bhjd,bhijd->bhij', qb, kb, decay)
        o = o_inter + jnp.einsum('bhij,bhje->bhie', att, vb)
        b_last = b[:, :, -1:, :]
        s_new = jnp.exp(b_last[:, :, 0, :, None]) * s + jnp.einsum('bhld,bhle->bhde', kb * jnp.exp(b_last - b), vb)
        return s_new, o

    s_fin, o = lax.scan(step, s0, (blocks(q), blocks(k), blocks(v), blocks(log_a)))
    o = o.transpose(1, 0, 3, 2, 4).reshape(B, T, H, DV)
    return o, s_fin


def rel_bias_matrix(rel_bias, q_rel, k_rel):
    dist = q_rel[:, None] - k_rel[None, :]
    idx = jnp.clip(dist, -MAX_REL, MAX_REL) + MAX_REL
    return rel_bias[:, idx].astype(jnp.float32)


def band_softmax_attention(qb, kb, vb, bias, valid):
    s = jnp.einsum('bnqhd,bnkhd->bnhqk', qb, kb).astype(jnp.float32) * (ATT_DH ** -0.5) + bias[None, None]
    s = jnp.where(valid[None, :, None, None, :], s, -jnp.inf)
    p = jax.nn.softmax(s, axis=-1).astype(vb.dtype)
    return jnp.einsum('bnhqk,bnkhd->bnqhd', p, vb)


def chunk_band_prompt(q, k, v, rel_bias):
    B, T, H, DH = q.shape
    n = T // CHUNK

    def band(a):
        ap = jnp.pad(a, ((0, 0), (ATT_WIN, 0), (0, 0), (0, 0))).reshape(B, n + LEFT_CHUNKS, CHUNK, H, DH)
        return jnp.concatenate([ap[:, c:c + n] for c in range(LEFT_CHUNKS + 1)], axis=2)

    q_rel = jnp.arange(CHUNK)
    k_rel = jnp.arange((LEFT_CHUNKS + 1) * CHUNK) - ATT_WIN
    valid = (jnp.arange(n)[:, None] * CHUNK + k_rel[None, :]) >= 0
    o = band_softmax_attention(q.reshape(B, n, CHUNK, H, DH), band(k), band(v),
                               rel_bias_matrix(rel_bias, q_rel, k_rel), valid)
    return o.reshape(B, T, H * DH)


def chunk_band_sample(q, k, v, k_cache, v_cache, rel_bias):
    B, T, H, DH = q.shape
    W = k_cache.shape[1]
    keys = jnp.concatenate([k_cache.astype(k.dtype), k], axis=1)[:, None]
    vals = jnp.concatenate([v_cache.astype(v.dtype), v], axis=1)[:, None]
    q_rel = jnp.arange(T)
    k_rel = jnp.concatenate([jnp.arange(W) - W, jnp.arange(T)])
    valid = jnp.ones((1, W + T), dtype=bool)
    o = band_softmax_attention(q[:, None], keys, vals, rel_bias_matrix(rel_bias, q_rel, k_rel), valid)
    return o.reshape(B, T, H * DH)


def hybrid_layer(x, state, pos0, lw):
    (attn_norm_g, w_in, w_gate2, b_gate, gla_norm_g, pool_map, pool_scale, rel_bias,
     w_branch, w_out, ffn_norm_g, w_ffn_in, w_ffn_out) = lw
    B, T, _ = x.shape
    f32 = jnp.float32
    xn = rms_norm(x, attn_norm_g)
    h = xn @ w_in
    u_a, q_b, k_b, v_b, r_b, z_b, q_c, k_c, v_c, gate_logits = jnp.split(h, in_proj_offsets(), axis=-1)

    if state is None:
        pool_hist = jnp.zeros((B, POOL_HIST, POOL_WIDTH), x.dtype)
        s0 = jnp.zeros((B, GLA_HEADS, GLA_DK, GLA_DV), f32)
    else:
        pool_hist, s0, k_cache, v_cache = state
        s0 = s0.astype(f32)

    y_a, pool_new = pool_mixer(u_a, pool_hist, pos0, pool_map, pool_scale)

    log_a = jax.nn.log_sigmoid((z_b @ w_gate2 + b_gate).astype(f32)) / GLA_TAU
    qh = q_b.astype(f32).reshape(B, T, GLA_HEADS, GLA_DK) * (GLA_DK ** -0.5)
    kh = k_b.astype(f32).reshape(B, T, GLA_HEADS, GLA_DK)
    vh = v_b.astype(f32).reshape(B, T, GLA_HEADS, GLA_DV)
    o_b, s_fin = gla_recurrence(qh, kh, vh, log_a.reshape(B, T, GLA_HEADS, GLA_DK), s0)
    o_b = o_b * lax.rsqrt(jnp.mean(o_b * o_b, axis=-1, keepdims=True) + EPS) * gla_norm_g.astype(f32)
    y_b = (o_b.reshape(B, T, GLA_V_W) * jax.nn.silu(r_b.astype(f32))).astype(x.dtype)

    qa = q_c.reshape(B, T, ATT_HEADS, ATT_DH)
    ka = k_c.reshape(B, T, ATT_HEADS, ATT_DH)
    va = v_c.reshape(B, T, ATT_HEADS, ATT_DH)
    if state is None:
        y_c = chunk_band_prompt(qa, ka, va, rel_bias)
        keep = min(ATT_WIN, T)
        k_new, v_new = ka[:, T - keep:], va[:, T - keep:]
    else:
        y_c = chunk_band_sample(qa, ka, va, k_cache, v_cache, rel_bias)
        k_new, v_new = ka, va

    ys = jnp.stack([y_a, y_b, y_c.astype(x.dtype)], axis=2)
    branch = jnp.einsum('btgc,gcd->btgd', ys, w_branch)
    gates = jax.nn.sigmoid(gate_logits.reshape(B, T, N_BRANCH, D_MODEL))
    x = x + jnp.sum(gates * branch, axis=2) @ w_out

    hf = rms_norm(x, ffn_norm_g) @ w_ffn_in
    a, g = jnp.split(hf, 2, axis=-1)
    x = x + (jax.nn.silu(a) * g) @ w_ffn_out
    return x, (pool_new, s_fin.astype(x.dtype), k_new, v_new)


def setup_inputs(seed: int = 0) -> dict:
    key = jax.random.key(seed)
    ks = jax.random.split(key, 20)
    f32 = jnp.float32

    def nrm(k, shape, scale):
        return jax.random.normal(k, shape, f32) * scale

    c_win = min(ATT_WIN, PAST_LEN)
    return {
        'x_prompt': nrm(ks[0], (BATCH, SEQ, D_MODEL), 1.0),
        'x_sample': nrm(ks[1], (DEC_BATCH, DEC_SEQ, D_MODEL), 1.0),
        'cache_pool': nrm(ks[2], (DEPTH, DEC_BATCH, POOL_HIST, POOL_WIDTH), 1.0),
        'state_gla': nrm(ks[3], (DEPTH, DEC_BATCH, GLA_HEADS, GLA_DK, GLA_DV), 1.0),
        'cache_k': nrm(ks[4], (DEPTH, DEC_BATCH, c_win, ATT_HEADS, ATT_DH), 1.0),
        'cache_v': nrm(ks[5], (DEPTH, DEC_BATCH, c_win, ATT_HEADS, ATT_DH), 1.0),
        'attn_norm_g': 1.0 + nrm(ks[6], (DEPTH, D_MODEL), 0.1),
        'w_in': nrm(ks[7], (DEPTH, D_MODEL, D_IN), D_MODEL ** -0.5),
        'w_gate2': nrm(ks[8], (DEPTH, GLA_RANK, GLA_QK_W), GLA_RANK ** -0.5),
        'b_gate': nrm(ks[9], (DEPTH, GLA_QK_W), 0.1),
        'gla_norm_g': 1.0 + nrm(ks[10], (DEPTH, GLA_DV), 0.1),
        'pool_map': nrm(ks[11], (DEPTH, POOL_GROUPS, POOL_GW, POOL_GW), POOL_GW ** -0.5),
        'pool_scale': 1.0 + nrm(ks[12], (DEPTH, POOL_WIDTH), 0.1),
        'rel_bias': nrm(ks[13], (DEPTH, ATT_HEADS, 2 * MAX_REL + 1), 0.5),
        'w_branch': nrm(ks[14], (DEPTH, N_BRANCH, MIX_BRANCH, D_MODEL), MIX_BRANCH ** -0.5),
        'w_out': nrm(ks[15], (DEPTH, D_MODEL, D_MODEL), D_MODEL ** -0.5),
        'ffn_norm_g': 1.0 + nrm(ks[16], (DEPTH, D_MODEL), 0.1),
        'w_ffn_in': nrm(ks[17], (DEPTH, D_MODEL, 2 * D_FF), D_MODEL ** -0.5),
        'w_ffn_out': nrm(ks[18], (DEPTH, D_FF, D_MODEL), D_FF ** -0.5),
        'final_norm_g': 1.0 + nrm(ks[19], (D_MODEL,), 0.1),
    }


def reference(x_prompt, x_sample, cache_pool, state_gla, cache_k, cache_v, attn_norm_g, w_in, w_gate2, b_gate,
              gla_norm_g, pool_map, pool_scale, rel_bias, w_branch, w_out, ffn_norm_g, w_ffn_in, w_ffn_out,
              final_norm_g):
    hp, hs = x_prompt, x_sample
    pool_p, pool_s, gla_p, gla_s, k_p, k_s, v_p, v_s = [], [], [], [], [], [], [], []
    for l in range(DEPTH):
        lw = (attn_norm_g[l], w_in[l], w_gate2[l], b_gate[l], gla_norm_g[l], pool_map[l], pool_scale[l],
              rel_bias[l], w_branch[l], w_out[l], ffn_norm_g[l], w_ffn_in[l], w_ffn_out[l])
        hp, (pp, gp, kp, vp) = hybrid_layer(hp, None, 0, lw)
        hs, (ps, gs, kss, vss) = hybrid_layer(hs, (cache_pool[l], state_gla[l], cache_k[l], cache_v[l]), PAST_LEN, lw)
        pool_p.append(pp); gla_p.append(gp); k_p.append(kp); v_p.append(vp)
        pool_s.append(ps); gla_s.append(gs); k_s.append(kss); v_s.append(vss)
    y_prompt = rms_norm(hp, final_norm_g)
    y_sample = rms_norm(hs, final_norm_g)
    new_pool_prompt = jnp.stack(pool_p)
    new_pool_sample = jnp.stack(pool_s)
    new_gla_prompt = jnp.stack(gla_p)
    new_gla_sample = jnp.stack(gla_s)
    new_k_prompt = jnp.stack(k_p)
    new_k_sample = jnp.stack(k_s)
    new_v_prompt = jnp.stack(v_p)
    new_v_sample = jnp.stack(v_s)
    return (y_prompt, y_sample, new_pool_prompt, new_pool_sample, new_gla_prompt, new_gla_sample,
            new_k_prompt, new_k_sample, new_v_prompt, new_v_sample)
```

```python
import numpy as np
from contextlib import ExitStack
import concourse.bass as bass
import concourse.mybir as mybir
from concourse.bass_utils import run_bass_kernel_spmd

F32 = mybir.dt.float32
BF16 = mybir.dt.bfloat16
I32 = mybir.dt.int32
AF = mybir.ActivationFunctionType
ALU = mybir.AluOpType
AX = mybir.AxisListType

ENGS = ("pe", "act", "dve", "pool", "sp")
DMAQ = {"sp": 16, "pool": 8, "act": 2}
NCORES = 8


class Op:
    __slots__ = ("eng", "fn", "deps", "needed", "sem", "val", "is_dma", "inc")

    def __init__(self, eng, fn, is_dma):
        self.eng = eng
        self.fn = fn
        self.deps = []
        self.needed = False
        self.sem = None
        self.val = 0
        self.is_dma = is_dma
        self.inc = 0


class Prog:
    SAME_ENGINE_SYNC = True

    def __init__(self, nc, es):
        self.nc = nc
        self.es = es
        self.ops = {e: [] for e in ENGS}
        self.lastc = {e: None for e in ENGS}
        self.res = {}
        self.bar = {e: None for e in ENGS}
        self.esem = {e: es.enter_context(nc.semaphore("s_" + e)) for e in ENGS}
        self.dsem = {q: [es.enter_context(nc.semaphore("d_%s%d" % (q, i))) for i in range(n)]
                     for q, n in DMAQ.items()}
        self.dlast = {q: [None] * n for q, n in DMAQ.items()}
        self.drr = {q: 0 for q in DMAQ}

    def sb(self, name, shape, dt):
        return self.es.enter_context(self.nc.sbuf_tensor(name, list(shape), dt))

    def ps(self, name, shape, dt=F32):
        return self.es.enter_context(self.nc.psum_tensor(name, list(shape), dt))

    def barrier(self):
        deps = []
        for e in ENGS:
            if self.lastc[e] is not None:
                deps.append(self.lastc[e])
        for q in DMAQ:
            deps += [o for o in self.dlast[q] if o is not None]
        for e in ENGS:
            self.bar[e] = list(deps)

    def _rec(self, eng, fn, reads, writes, is_dma, nobar=False, serial=False):
        op = Op(eng, fn, is_dma)
        deps = []
        forced = self.lastc[eng] if serial else None
        if not nobar and self.bar[eng] is not None:
            deps.extend(self.bar[eng])
            self.bar[eng] = None
        for r in reads:
            st = self.res.get(r)
            if st is not None and st[0] is not None:
                deps.append(st[0])
        for w in writes:
            st = self.res.get(w)
            if st is not None:
                if st[0] is not None:
                    deps.append(st[0])
                deps.extend(st[1])
        if is_dma:
            q = eng
            i = self.drr[q]
            self.drr[q] = (i + 1) % len(self.dsem[q])
            prev = self.dlast[q][i]
            if prev is not None:
                deps.append(prev)
            self.dlast[q][i] = op
            op.sem = self.dsem[q][i]
            op.inc = 16
            op.val = (prev.val if prev is not None else 0) + 16
            op.needed = True
        seen = set()
        for d in deps:
            if id(d) in seen or d is op:
                continue
            seen.add(id(d))
            if (not d.is_dma) and d.eng == eng and (eng == "pe" or not self.SAME_ENGINE_SYNC):
                continue
            op.deps.append(d)
            d.needed = True
        if forced is not None and all(forced is not d for d in op.deps):
            d = forced
            op.deps.append(d)
            d.needed = True
        for w in writes:
            self.res[w] = [op, []]
        for r in reads:
            st = self.res.setdefault(r, [None, []])
            if not is_dma:
                st[1] = [o for o in st[1] if o.is_dma or o.eng != eng]
            st[1].append(op)
        self.ops[eng].append(op)
        if not is_dma:
            self.lastc[eng] = op
        return op

    def op(self, eng, fn, reads=(), writes=(), serial=False):
        return self._rec(eng, fn, reads, writes, False, serial=serial)

    def dma(self, q, out, in_, reads=(), writes=(), nobar=False):
        return self._rec(q, lambda e: e.dma_start(out=out, in_=in_), reads, writes, True, nobar)

    def dma_fn(self, q, fn, reads=(), writes=()):
        return self._rec(q, fn, reads, writes, True)

    def replay(self, block):
        for e in ENGS:
            c = 0
            for op in self.ops[e]:
                if op.is_dma:
                    continue
                if op.needed:
                    c += 1
                    op.sem = self.esem[e]
                    op.val = c
                    op.inc = 1
        last_dma = {}
        for e in ENGS:
            for op in self.ops[e]:
                if op.is_dma:
                    last_dma[id(op.sem)] = op

        def run(ename, eng):
            seen = {}
            for op in self.ops[ename]:
                need = {}
                for d in op.deps:
                    k = id(d.sem)
                    if seen.get(k, 0) < d.val and need.get(k, (None, 0))[1] < d.val:
                        need[k] = (d.sem, d.val)
                for k, (s, v) in need.items():
                    eng.wait_ge(s, v)
                    seen[k] = v
                ins = op.fn(eng)
                if op.inc:
                    ins.then_inc(op.sem, op.inc)
            if ename == "sp":
                for op in last_dma.values():
                    if seen.get(id(op.sem), 0) < op.val:
                        eng.wait_ge(op.sem, op.val)

        @block.tensor
        def _(e):
            run("pe", e)

        @block.scalar
        def _(e):
            run("act", e)

        @block.vector
        def _(e):
            run("dve", e)

        @block.gpsimd
        def _(e):
            run("pool", e)

        @block.sync
        def _(e):
            run("sp", e)


class Arena:
    def __init__(self, tile, nbytes):
        self.t = tile
        self.n = nbytes
        self.off = 0
        self.phase = 0

    def reset(self):
        self.off = 0
        self.phase += 1

    def alloc(self, name, shape, dt):
        esz = 4 if dt in (F32, I32) else 2
        nfree = 1
        for s in shape[1:]:
            nfree *= s
        n = nfree * esz
        o = self.off
        self.off += (n + 31) // 32 * 32
        assert self.off <= self.n, ("arena overflow", name, self.off, self.n)
        v = self.t[:, o // 2:(o + n) // 2]
        if esz == 4:
            v = v.bitcast(dt)
        if len(shape) == 3:
            v = v.rearrange("p (a b) -> p a b", a=shape[1])
        elif len(shape) == 4:
            v = v.rearrange("p (a b c) -> p a b c", a=shape[1], b=shape[2])
        if shape[0] < 128:
            v = v[0:shape[0]]
        return v, "AR%d_%s" % (self.phase, name)


D = 1024
DIN = 6672
DFF = 2816
NL = 2
T = 2048
SEG = 512
NSEG = 4
NS = 32
LW = SEG + NS
OFF = dict(ua=0, qb=512, kb=768, vb=1024, rb=1536, zb=2048, qc=2064, kc=2576, vc=3088, gate=3600)
NEG = -30000.0
ARENA_BYTES = 57344
WSLOT = 8448
NWS = 3


def build():
    nc = bass.Bass("TRN2", target_bir_lowering=False)

    def din(name, shape, dt=F32):
        return nc.dram_tensor(name, list(shape), dt, kind="ExternalInput").ap()

    def dout(name, shape):
        return nc.dram_tensor(name, list(shape), F32, kind="ExternalOutput").ap()

    xin = din("xin", [2560, D])
    xs_in = din("xs", [NS, D])
    cpool = din("cpool", [NL, 15, 512])
    sgla = din("sgla", [NL, 4, 64, 128])
    ck = din("ck", [NL, 512, 512])
    cv = din("cv", [NL, 512, 512])
    ang = din("ang", [128, NL, 8])
    fng = din("fng", [128, NL, 8])
    fing = din("fing", [128, 8])
    w_in = din("w_in", [NL, D, DIN])
    wg2 = din("wg2", [16, NL, 256])
    bgate = din("bgate", [128, NL, 2])
    gnorm = din("gnorm", [128, NL, 1])
    pmap = din("pmap", [128, NL, 4, 128])
    pscale = din("pscale", [128, NL, 4])
    w_br = din("w_br", [NL, 3, 512, D])
    w_out = din("w_out", [NL, D, D])
    w_f1 = din("w_f1", [NL, D, 2 * DFF])
    w_f2 = din("w_f2", [NL, DFF, D])
    bias2 = din("bias2", [NL, 128, 8, 640])
    biass = din("biass", [NL, NS, 8, 544])
    kmask_in = din("kmask", [128, 4, 640])
    rc_in = din("rc", [128, 4, 16])
    mvec_in = din("mvec", [128, 8])
    gidx_in = din("gidx", [128, 8], I32)

    y_out = dout("y", [T, D])
    ys_out = dout("ys", [NS, D])
    o_pool_p = dout("o_pool_p", [NL, 15, 512])
    o_pool_s = dout("o_pool_s", [NL, 15, 512])
    o_gla_p = dout("o_gla_p", [NL, 4, 64, 128])
    o_gla_s = dout("o_gla_s", [NL, 4, 64, 128])
    o_k_p = dout("o_k_p", [NL, 512, 512])
    o_k_s = dout("o_k_s", [NL, NS, 512])
    o_v_p = dout("o_v_p", [NL, 512, 512])
    o_v_s = dout("o_v_s", [NL, NS, 512])

    cc1_in = [nc.dram_tensor("cc1_in%d" % g, [256, 256], F32) for g in range(4)]
    cc1_out = [nc.dram_tensor("cc1_out%d" % g, [NCORES * 256, 256], F32) for g in range(4)]
    cc2_in = [nc.dram_tensor("cc2_in%d" % l, [128, 520], F32) for l in range(NL)]
    cc2_out = [nc.dram_tensor("cc2_out%d" % l, [NCORES * 128, 520], F32) for l in range(NL)]

    with ExitStack() as es:
        P = Prog(nc, es)
        xT = P.sb("xT", [128, 8, T + NS], F32)
        xn = P.sb("xn", [128, 8, LW], BF16)
        rstd = P.sb("rstd", [128, LW], F32)
        ycT = P.sb("ycT", [128, 4, LW], BF16)
        ybT = P.sb("ybT", [128, 4, LW], BF16)
        yaT = P.sb("yaT", [128, 4, LW], BF16)
        kT = P.sb("kT", [128, 4, 1024], BF16)
        vtok = P.sb("vtok", [128, 8, 512], BF16)
        ua = P.sb("ua", [128, 4, 16 + SEG], F32)
        Sst = P.sb("Sst", [128, 2, 256], F32)
        Sb = P.sb("Sb", [128, 2, 256], BF16)
        Ssm = P.sb("Ssm", [128, 2, 256], F32)
        Sbs = P.sb("Sbs", [128, 2, 256], BF16)
        Strue = P.sb("Strue", [128, 2, 256], F32)
        Dacc = P.sb("Dacc", [128, 2], F32)
        identF = P.sb("identF", [128, 128], F32)
        identB = P.sb("identB", [128, 128], BF16)
        tri = P.sb("tri", [64, 4, 64], F32)
        onesD = P.sb("onesD", [128, 128], BF16)
        onesV = P.sb("onesV", [128, 128], BF16)
        c_ang = P.sb("c_ang", [128, NL, 8], F32)
        c_fng = P.sb("c_fng", [128, NL, 8], F32)
        c_fing = P.sb("c_fing", [128, 8], F32)
        c_negb = P.sb("c_negb", [128, NL, 2], F32)
        c_gn = P.sb("c_gn", [128, NL, 1], F32)
        c_ps = P.sb("c_ps", [128, NL, 4], F32)
        c_wg2 = P.sb("c_wg2", [16, NL, 256], BF16)
        c_pmap = P.sb("c_pmap", [128, NL, 4, 128], BF16)
        c_rc = P.sb("c_rc", [128, 4, 16], F32)
        c_mvec = P.sb("c_mvec", [128, 8], F32)
        c_gidx = P.sb("c_gidx", [128, 8], I32)
        wring = [P.sb("wring%d" % i, [128, WSLOT // 2], BF16) for i in range(NWS)]
        arena_t = P.sb("arena", [128, ARENA_BYTES // 2], BF16)
        AR = Arena(arena_t, ARENA_BYTES)
        pbs = [P.ps("pb%d" % i, [128, 512], F32) for i in range(8)]
        st = {"w": 0, "pb": 0}

        def MM(out, lhsT, rhs, s0, s1, R, W, serial=False):
            P.op("pe", lambda e: e.matmul(out, lhsT=lhsT, rhs=rhs, start=s0, stop=s1), R, W, serial=serial)

        def TR(out, in_, idn, R, W):
            P.op("pe", lambda e: e.transpose(out, in_, idn), R, W)

        def ACT(out, in_, func, R, W, **kw):
            P.op("act", lambda e: e.activation(out, in_, func, **kw), R, W)

        def ACOPY(out, in_, R, W):
            P.op("act", lambda e: e.copy(out, in_), R, W)

        def VCOPY(out, in_, R, W):
            P.op("dve", lambda e: e.tensor_copy(out, in_), R, W)

        def VTT(out, a, b, op, R, W):
            P.op("dve", lambda e: e.tensor_tensor(out, a, b, op), R, W)

        def VTS(out, a, s1, s2, op0, op1, R, W):
            if s2 is None:
                P.op("dve", lambda e: e.tensor_scalar(out, a, s1, None, op0), R, W)
            else:
                P.op("dve", lambda e: e.tensor_scalar(out, a, s1, s2, op0, op1), R, W)

        def VSTT(out, a, s, b, op0, op1, R, W):
            P.op("dve", lambda e: e.scalar_tensor_tensor(out, a, s, b, op0, op1), R, W)

        def VMEMSET(ap, val, W):
            P.op("dve", lambda e: e.memset(ap, val), (), W)

        def wload(dram2d, row0, KC, col0, ncols):
            i = st["w"] % NWS
            st["w"] += 1
            assert KC * ncols * 2 <= WSLOT
            v = wring[i][:, 0:KC * ncols].rearrange("p (k c) -> p k c", k=KC)
            src = dram2d[row0:row0 + KC * 128, col0:col0 + ncols].rearrange("(k p) c -> p k c", p=128)
            P.dma("pool", v, src, (), ["w%d" % i], nobar=True)
            return v, "w%d" % i

        def bank():
            i = st["pb"] % 6
            st["pb"] += 1
            return pbs[i], "pb%d" % i

        P.op("pool", lambda e: e.memset(identF[:], 0.0), (), ["identF"])
        P.op("pool", lambda e: e.affine_select(out=identF[:], in_=identF[:], pattern=[[-1, 128]],
                                               compare_op=ALU.not_equal, fill=1.0, base=0, channel_multiplier=1),
             ["identF"], ["identF"])
        VCOPY(identB[:], identF[:], ["identF"], ["identB"])
        P.op("pool", lambda e: e.memset(tri[:], 1.0), (), ["tri"])
        P.op("pool", lambda e: e.affine_select(out=tri[:], in_=tri[:], pattern=[[0, 4], [1, 64]],
                                               compare_op=ALU.is_ge, fill=0.0, base=0, channel_multiplier=-1),
             ["tri"], ["tri"])
        VMEMSET(onesD[:], 1.0 / D, ["onesD"])
        VMEMSET(onesV[:], 1.0 / 128, ["onesV"])
        P.dma("sp", c_ang[:], ang, (), ["c_ang"])
        P.dma("sp", c_fng[:], fng, (), ["c_fng"])
        P.dma("sp", c_fing[:], fing, (), ["c_fing"])
        P.dma("sp", c_negb[:], bgate, (), ["c_negb"])
        VTS(c_negb[:], c_negb[:], -1.0, None, ALU.mult, None, ["c_negb"], ["c_negb"])
        P.dma("sp", c_gn[:], gnorm, (), ["c_gn"])
        P.dma("sp", c_ps[:], pscale, (), ["c_ps"])
        P.dma("pool", c_wg2[:], wg2, (), ["c_wg2"])
        P.dma("pool", c_pmap[:], pmap, (), ["c_pmap"])
        P.dma("sp", c_rc[:], rc_in, (), ["c_rc"])
        P.dma("sp", c_mvec[:], mvec_in, (), ["c_mvec"])
        P.dma("sp", c_gidx[:], gidx_in, (), ["c_gidx"])

        def seg_blocks(s, with_sample=True):
            b = [(s * SEG, SEG, 0)]
            if s == NSEG - 1 and with_sample:
                b.append((T, NS, SEG))
            return b

        def rmsnorm(blocks, gvec, out_t, out_res, sq, sq_res, xsrc=None, xsrc_res=None, f32out=None):
            for (xc0, n, lc0) in blocks:
                pb, pr = bank()
                for k in range(8):
                    src = xT[:, k, xc0:xc0 + n] if xsrc is None else xsrc[:, k, lc0:lc0 + n]
                    sres = "xT" if xsrc is None else xsrc_res
                    ACT(sq[:, k, lc0:lc0 + n], src, AF.Square, [sres], [sq_res])
                for k in range(8):
                    MM(pb[:, 0:n], onesD[:], sq[:, k, lc0:lc0 + n], k == 0, k == 7, ["onesD", sq_res], [pr])
                ACT(rstd[:, lc0:lc0 + n], pb[:, 0:n], AF.Ln, [pr], ["rstd"], bias=1e-6, scale=1.0)
                ACT(rstd[:, lc0:lc0 + n], rstd[:, lc0:lc0 + n], AF.Exp, ["rstd"], ["rstd"], scale=-0.5)
                for k in range(8):
                    src = xT[:, k, xc0:xc0 + n] if xsrc is None else xsrc[:, k, lc0:lc0 + n]
                    sres = "xT" if xsrc is None else xsrc_res
                    dst = out_t[:, k, lc0:lc0 + n] if f32out is None else f32out[:, k, lc0:lc0 + n]
                    VSTT(dst, src, gvec[:, k:k + 1], rstd[:, lc0:lc0 + n], ALU.mult, ALU.mult,
                         [sres, "rstd"], [out_res])

        def load_x_rows(dram_rows, nrows, dst_fn, dst_res):
            AR.reset()
            P.barrier()
            nt = (nrows + 127) // 128
            xtok, xr = AR.alloc("xtok", [128, nt, D], F32)
            if nrows >= 128:
                P.dma("sp", xtok[:], dram_rows.rearrange("(t p) f -> p t f", p=128), (), [xr])
            else:
                P.dma("sp", xtok[0:nrows, 0, :], dram_rows, (), [xr])
            for k in range(8):
                pb, pr = bank()
                for t in range(nt):
                    r = min(128, nrows - t * 128)
                    TR(pb[:, t * 128:t * 128 + r], xtok[0:r, t, k * 128:(k + 1) * 128], identF[0:r, 0:r],
                       [xr, "identF"], [pr])
                if k % 2 == 0:
                    VCOPY(dst_fn(k), pb[:, 0:nrows], [pr], [dst_res])
                else:
                    ACOPY(dst_fn(k), pb[:, 0:nrows], [pr], [dst_res])

        for s in range(NSEG):
            load_x_rows(xin[512 + s * SEG:512 + (s + 1) * SEG, :], SEG,
                        lambda k, s=s: xT[:, k, s * SEG:(s + 1) * SEG], "xT")
        load_x_rows(xs_in, NS, lambda k: xT[:, k, T:T + NS], "xT")

        def halo_proj(l, xh, xhr):
            wv, wr = wload(w_in[l], 0, 8, OFF["kc"], 512)
            for m in range(4):
                pb, pr = bank()
                for k in range(8):
                    MM(pb[:, 0:512], wv[:, k, m * 128:(m + 1) * 128], xh[:, k, :], k == 0, k == 7, [wr, xhr], [pr])
                ACOPY(kT[:, m, 0:512], pb[:, 0:512], [pr], ["kT"])
            wv, wr = wload(w_in[l], 0, 8, OFF["vc"], 512)
            for t in range(4):
                pb, pr = bank()
                for k in range(8):
                    MM(pb[:, 0:512], xh[:, k, t * 128:(t + 1) * 128], wv[:, k, :], k == 0, k == 7, [wr, xhr], [pr])
                VCOPY(vtok[:, t, :], pb[:, 0:512], [pr], ["vtok"])
            wv, wr = wload(w_in[l], 0, 8, OFF["ua"], 512)
            for m in range(4):
                pb, pr = bank()
                for k in range(8):
                    MM(pb[:, 0:16], wv[:, k, m * 128:(m + 1) * 128], xh[:, k, 496:512], k == 0, k == 7,
                       [wr, xhr], [pr])
                VCOPY(ua[:, m, 0:16], pb[:, 0:16], [pr], ["ua"])

        def gla_gate_prep(l, zT, zr, n, lc0, bp, bpr, bq, bqr, L):
            for c in range(2):
                pb, pr = bank()
                MM(pb[:, 0:n], c_wg2[:, l, c * 128:(c + 1) * 128], zT[:, lc0:lc0 + n], True, True,
                   ["c_wg2", zr], [pr])
                ACT(bq[:, c, lc0:lc0 + n], pb[:, 0:n], AF.Exp, [pr, "c_negb"], [bqr],
                    bias=c_negb[:, l, c:c + 1], scale=-1.0)
                ACT(bp[:, c, lc0:lc0 + n], bq[:, c, lc0:lc0 + n], AF.Ln, [bqr], [bpr], bias=1.0, scale=1.0)
            src, sres, dst, dres = bp, bpr, bq, bqr
            d = 1
            nch = n // L
            while d < L:
                for c in range(2):
                    sv = src[:, c, lc0:lc0 + n].rearrange("p (a b) -> p a b", b=L)
                    dv = dst[:, c, lc0:lc0 + n].rearrange("p (a b) -> p a b", b=L)
                    VCOPY(dv[:, :, 0:d], sv[:, :, 0:d], [sres], [dres])
                    VTT(dv[:, :, d:L], sv[:, :, d:L], sv[:, :, 0:L - d], ALU.add, [sres], [dres])
                src, sres, dst, dres = dst, dres, src, sres
                d *= 2
            return src, sres, dst, dres

        def gla_chunk(ch_c0, L, kt, ktr, qt, qtr, v64, v64r, vslot, ktok, ktokr, eb, ebr, S, Sres, Sbf, Sbres,
                      oT, oTr, compute_o, dacc):
            cs = slice(ch_c0, ch_c0 + L)
            if compute_o:
                pa, par = bank()
                for h in range(4):
                    c, e = h // 2, h % 2
                    MM(pa[0:L, h * 64:h * 64 + L], kt[e * 64:(e + 1) * 64, c, cs], qt[e * 64:(e + 1) * 64, c, cs],
                       True, True, [ktr, qtr], [par], serial=True)
                attb, attr = gla_tmp["attb"]
                if L == 64:
                    VTT(attb[0:L, :, :], pa[0:L, 0:256].rearrange("p (h i) -> p h i", h=4), tri[0:L, :, 0:L],
                        ALU.mult, [par, "tri"], [attr])
                else:
                    for h in range(4):
                        VTT(attb[0:L, h, 0:L], pa[0:L, h * 64:h * 64 + L], tri[0:L, h, 0:L], ALU.mult,
                            [par, "tri"], [attr])
                po, por = bank()
                for h in range(4):
                    c, e = h // 2, h % 2
                    MM(po[:, h * 64:h * 64 + L], v64[0:L, vslot, h * 128:(h + 1) * 128], attb[0:L, h, 0:L],
                       True, False, [v64r, attr], [por], serial=True)
                    MM(po[:, h * 64:h * 64 + L], Sbf[e * 64:(e + 1) * 64, c, e * 128:(e + 1) * 128],
                       qt[e * 64:(e + 1) * 64, c, cs], False, True, [Sbres, qtr], [por], serial=True)
                ACOPY(oT[:, :, cs], po[:, 0:256].rearrange("p (h i) -> p h i", h=4)[:, :, 0:L], [por], [oTr])
            pS, pSr = bank()
            for c in range(2):
                MM(pS[:, c * 256:(c + 1) * 256], ktok[0:L, vslot, c * 128:(c + 1) * 128],
                   v64[0:L, vslot, c * 256:(c + 1) * 256], True, True, [ktokr, v64r], [pSr])
            tmpS, tmpr = gla_tmp["tmpS"]
            last = ch_c0 + L - 1
            for c in range(2):
                ec = eb[:, c, last:last + 1]
                VTS(tmpS[:, c, :], S[:, c, :], ec, None, ALU.mult, None, [Sres, ebr], [tmpr])
                VSTT(S[:, c, :], pS[:, c * 256:(c + 1) * 256], ec, tmpS[:, c, :], ALU.mult, ALU.add,
                     [pSr, ebr, tmpr], [Sres])
            VCOPY(Sbf[:], S[:], [Sres], [Sbres])
            if dacc:
                VTT(Dacc[:], Dacc[:], eb[:, :, last], ALU.mult, ["Dacc", ebr], ["Dacc"])

        gla_tmp = {}

        def gla_segment(l, s, compute_o):
            AR.reset()
            P.barrier()
            blocks = seg_blocks(s, with_sample=compute_o)
            kg, kgr = AR.alloc("kg", [128, 2, LW], BF16)
            zT, zr = AR.alloc("zT", [16, LW], BF16)
            v64, v64r = AR.alloc("v64", [64, 9, 512], BF16)
            bp, bpr = AR.alloc("bp", [128, 2, LW], F32)
            bq, bqr = AR.alloc("bq", [128, 2, LW], F32)
            eb, ebr = AR.alloc("eb", [128, 2, LW], F32)
            kt, ktr = AR.alloc("kt", [128, 2, LW], BF16)
            ktok, ktokr = AR.alloc("ktok", [64, 9, 256], BF16)
            gla_tmp["tmpS"] = AR.alloc("tmpS", [128, 2, 256], F32)
            if compute_o:
                qg, qgr = AR.alloc("qg", [128, 2, LW], BF16)
                qt, qtr = AR.alloc("qt", [128, 2, LW], BF16)
                sr, srr = AR.alloc("sr", [128, 4, LW], BF16)
                oT, oTr = AR.alloc("oT", [128, 4, LW], F32)
                osq, osqr = AR.alloc("osq", [128, 4, LW], BF16)
                gla_tmp["attb"] = AR.alloc("attb", [64, 4, 64], BF16)
            else:
                qg = qt = sr = oT = osq = None
                qgr = qtr = srr = oTr = osqr = None
            if compute_o:
                wv, wr = wload(w_in[l], 0, 8, OFF["qb"], 512)
                for m in range(4):
                    for (xc0, n, lc0) in blocks:
                        pb, pr = bank()
                        for k in range(8):
                            MM(pb[:, 0:n], wv[:, k, m * 128:(m + 1) * 128], xn[:, k, lc0:lc0 + n], k == 0, k == 7,
                               [wr, "xn"], [pr])
                        if m < 2:
                            ACT(qg[:, m, lc0:lc0 + n], pb[:, 0:n], AF.Copy, [pr], [qgr], scale=0.125)
                        else:
                            VCOPY(kg[:, m - 2, lc0:lc0 + n], pb[:, 0:n], [pr], [kgr])
            else:
                wv, wr = wload(w_in[l], 0, 8, OFF["kb"], 256)
                for m in range(2):
                    for (xc0, n, lc0) in blocks:
                        pb, pr = bank()
                        for k in range(8):
                            MM(pb[:, 0:n], wv[:, k, m * 128:(m + 1) * 128], xn[:, k, lc0:lc0 + n], k == 0, k == 7,
                               [wr, "xn"], [pr])
                        VCOPY(kg[:, m, lc0:lc0 + n], pb[:, 0:n], [pr], [kgr])
            wv, wr = wload(w_in[l], 0, 8, OFF["vb"], 512)
            for (xc0, n, lc0) in blocks:
                L = 64 if n == SEG else NS
                for ci in range(n // L):
                    slot = ci if n == SEG else 8
                    pb, pr = bank()
                    for k in range(8):
                        MM(pb[0:L, 0:512], xn[:, k, lc0 + ci * L:lc0 + (ci + 1) * L], wv[:, k, :], k == 0, k == 7,
                           [wr, "xn"], [pr])
                    if ci % 2 == 0:
                        VCOPY(v64[0:L, slot, :], pb[0:L, 0:512], [pr], [v64r])
                    else:
                        ACOPY(v64[0:L, slot, :], pb[0:L, 0:512], [pr], [v64r])
            if compute_o:
                wv, wr = wload(w_in[l], 0, 8, OFF["rb"], 528)
                nm = 5
            else:
                wv, wr = wload(w_in[l], 0, 8, OFF["zb"], 16)
                nm = 1
            for m in range(nm):
                for (xc0, n, lc0) in blocks:
                    pb, pr = bank()
                    isz = (m == nm - 1)
                    msz = 16 if isz else 128
                    moff = (512 if compute_o else 0) if isz else m * 128
                    for k in range(8):
                        MM(pb[0:msz, 0:n], wv[:, k, moff:moff + msz], xn[:, k, lc0:lc0 + n], k == 0, k == 7,
                           [wr, "xn"], [pr])
                    if isz:
                        VCOPY(zT[:, lc0:lc0 + n], pb[0:16, 0:n], [pr], [zr])
                    else:
                        ACT(sr[:, m, lc0:lc0 + n], pb[:, 0:n], AF.Silu, [pr], [srr])
            if compute_o and GSTOP <= 1:
                return
            for (xc0, n, lc0) in blocks:
                L = 64 if n == SEG else NS
                cum, cumr, scr, scrr = gla_gate_prep(l, zT, zr, n, lc0, bp, bpr, bq, bqr, L)
                cs = slice(lc0, lc0 + n)
                for c in range(2):
                    ACT(scr[:, c, cs], cum[:, c, cs], AF.Exp, [cumr], [scrr], scale=1.0 / 16)
                    VTT(kt[:, c, cs], kg[:, c, cs], scr[:, c, cs], ALU.mult, [kgr, scrr], [ktr])
                    ACT(eb[:, c, cs], cum[:, c, cs], AF.Exp, [cumr], [ebr], scale=-1.0 / 16)
                    if compute_o:
                        VTT(qt[:, c, cs], qg[:, c, cs], eb[:, c, cs], ALU.mult, [qgr, ebr], [qtr])
                nch = n // L
                pbt = pbs[6][:, :].bitcast(BF16)
                for ci in range(nch):
                    slot = ci if n == SEG else 8
                    half = ci % 4
                    for c in range(2):
                        TR(pbt[0:L, half * 256 + c * 128:half * 256 + (c + 1) * 128],
                           kt[:, c, lc0 + ci * L:lc0 + (ci + 1) * L], identB[:], [ktr, "identB"], ["pb6"])
                    if half == 3 or ci == nch - 1:
                        c0 = ci - half
                        s0 = c0 if n == SEG else 8
                        VCOPY(ktok[0:L, s0:s0 + half + 1, :],
                              pbt[0:L, 0:(half + 1) * 256].rearrange("p (a b) -> p a b", b=256), ["pb6"], [ktokr])
                if compute_o and GSTOP <= 2:
                    return
                if n == SEG:
                    S, Sres, Sbf, Sbres = Sst, "Sst", Sb, "Sb"
                else:
                    S, Sres, Sbf, Sbres = Ssm, "Ssm", Sbs, "Sbs"
                for ci in range(nch):
                    slot = ci if n == SEG else 8
                    gla_chunk(lc0 + ci * L, L, kt, ktr, qt, qtr, v64, v64r, slot, ktok, ktokr, eb, ebr,
                              S, Sres, Sbf, Sbres, oT, oTr, compute_o, dacc=(not compute_o))
                if compute_o and GSTOP <= 3:
                    return
                if compute_o:
                    for h in range(4):
                        ACT(osq[:, h, cs], oT[:, h, cs], AF.Square, [oTr], [osqr])
                        pb, pr = bank()
                        MM(pb[:, 0:n], onesV[:], osq[:, h, cs], True, True, ["onesV", osqr], [pr])
                        ACT(rstd[:, cs], pb[:, 0:n], AF.Ln, [pr], ["rstd"], bias=1e-6, scale=1.0)
                        ACT(rstd[:, cs], rstd[:, cs], AF.Exp, ["rstd"], ["rstd"], scale=-0.5)
                        VSTT(oT[:, h, cs], oT[:, h, cs], c_gn[:, l, 0:1], rstd[:, cs], ALU.mult, ALU.mult,
                             [oTr, "rstd", "c_gn"], [oTr])
                        VTT(ybT[:, h, cs], oT[:, h, cs], sr[:, h, cs], ALU.mult, [oTr, srr], ["ybT"])

        def load_state_sample(l):
            VMEMSET(Ssm[:], 0.0, ["Ssm"])
            for h in range(4):
                c, e = h // 2, h % 2
                P.dma("sp", Ssm[e * 64:(e + 1) * 64, c, e * 128:(e + 1) * 128], sgla[l, h], (), ["Ssm"])
            VCOPY(Sbs[:], Ssm[:], ["Ssm"], ["Sbs"])

        def store_state(S, Sres, dst):
            for h in range(4):
                c, e = h // 2, h % 2
                P.dma("sp", dst[h], S[e * 64:(e + 1) * 64, c, e * 128:(e + 1) * 128], [Sres], ())

        def gla_exchange(l):
            AR.reset()
            P.barrier()
            ex, exr = AR.alloc("ex", [128, 520], F32)
            G, Gr = AR.alloc("G", [128, 8, 520], F32)
            dsel, dselr = AR.alloc("dsel", [128, 8, 2], F32)
            VMEMSET(ex[:], 0.0, [exr])
            VCOPY(ex[:, 0:512], Sst[:].rearrange("p a b -> p (a b)"), ["Sst"], [exr])
            VCOPY(ex[:, 512:514], Dacc[:], ["Dacc"], [exr])
            P.dma("sp", cc2_in[l].ap(), ex[:], [exr], ["cc2_in"])
            P.op("pool", lambda e: e.collective_compute("AllGather", ALU.bypass,
                                                        replica_groups=[list(range(NCORES))],
                                                        ins=[cc2_in[l].ap().opt()], outs=[cc2_out[l].ap().opt()]),
                 ["cc2_in"], ["cc2_out"])
            P.dma("sp", G[:], cc2_out[l].ap().rearrange("(r p) c -> p r c", p=128), ["cc2_out"], [Gr])
            for c in range(2):
                VTS(dsel[:, :, c], G[:, :, 512 + c], -1.0, None, ALU.add, None, [Gr], [dselr])
                VTT(dsel[:, :, c], dsel[:, :, c], c_mvec[:], ALU.mult, [dselr, "c_mvec"], [dselr])
                VTS(dsel[:, :, c], dsel[:, :, c], 1.0, None, ALU.add, None, [dselr], [dselr])
            VMEMSET(Strue[:], 0.0, ["Strue"])
            for j in range(NCORES):
                for c in range(2):
                    VTS(Strue[:, c, :], Strue[:, c, :], dsel[:, j, c:c + 1], None, ALU.mult, None,
                        ["Strue", dselr], ["Strue"])
                    VSTT(Strue[:, c, :], G[:, j, c * 256:(c + 1) * 256], c_mvec[:, j:j + 1], Strue[:, c, :],
                         ALU.mult, ALU.add, [Gr, "c_mvec", "Strue"], ["Strue"])

        def attention_segment(l, s):
            AR.reset()
            P.barrier()
            blocks = seg_blocks(s)
            sq, sqr = AR.alloc("sq", [128, 8, LW], BF16)
            qT, qTr = AR.alloc("qT", [128, 4, LW], BF16)
            B2, B2r = AR.alloc("B2", [128, 8, 640], F32)
            s1, s1r = AR.alloc("s1", [128, 640], F32)
            Pb, Pbr = AR.alloc("Pb", [128, 640], BF16)
            PT, PTr = AR.alloc("PT", [128, 5, 128], BF16)
            yct, yctr = AR.alloc("yct", [128, 512], BF16)
            stat, statr = AR.alloc("stat", [128, 4], F32)
            if s == 0:
                km, kmr = AR.alloc("km", [128, 4, 640], F32)
                P.dma("sp", km[:], kmask_in, (), [kmr])
            if s == NSEG - 1:
                kTs, kTsr = AR.alloc("kTs", [128, 4, 544], BF16)
                vts, vtsr = AR.alloc("vts", [128, 5, 512], BF16)
                ckb, ckbr = AR.alloc("ckb", [128, 4, 512], BF16)
                kout, koutr = AR.alloc("kout", [128, 512], F32)
            P.dma("sp", B2[:], bias2[l], (), [B2r])
            rmsnorm(blocks, c_ang[:, l, :], xn, "xn", sq, sqr)
            for nm_, off, dst, dres, dcol in (("q", OFF["qc"], qT, qTr, 0), ("k", OFF["kc"], kT, "kT", 512)):
                wv, wr = wload(w_in[l], 0, 8, off, 512)
                for m in range(4):
                    for (xc0, n, lc0) in blocks:
                        pb, pr = bank()
                        for k in range(8):
                            MM(pb[:, 0:n], wv[:, k, m * 128:(m + 1) * 128], xn[:, k, lc0:lc0 + n], k == 0, k == 7,
                               [wr, "xn"], [pr])
                        if n == SEG:
                            (ACOPY if m % 2 else VCOPY)(dst[:, m, dcol:dcol + n], pb[:, 0:n], [pr], [dres])
                        elif nm_ == "q":
                            VCOPY(qT[:, m, SEG:SEG + NS], pb[:, 0:n], [pr], [qTr])
                        else:
                            VCOPY(kTs[:, m, 512:544], pb[:, 0:n], [pr], [kTsr])
                if nm_ == "k" and s == NSEG - 1:
                    for t in range(4):
                        pb, pr = bank()
                        for k in range(8):
                            MM(pb[:, 0:512], xn[:, k, t * 128:(t + 1) * 128], wv[:, k, :], k == 0, k == 7,
                               [wr, "xn"], [pr])
                        VCOPY(kout[:], pb[:, 0:512], [pr], [koutr])
                        P.dma("sp", o_k_p[l, t * 128:(t + 1) * 128, :], kout[:], [koutr], ())
                    pb, pr = bank()
                    for k in range(8):
                        MM(pb[0:NS, 0:512], xn[:, k, SEG:SEG + NS], wv[:, k, :], k == 0, k == 7, [wr, "xn"], [pr])
                    VCOPY(kout[0:NS, :], pb[0:NS, 0:512], [pr], [koutr])
                    P.dma("sp", o_k_s[l], kout[0:NS, :], [koutr], ())
            wv, wr = wload(w_in[l], 0, 8, OFF["vc"], 512)
            for t in range(4):
                pb, pr = bank()
                for k in range(8):
                    MM(pb[:, 0:512], xn[:, k, t * 128:(t + 1) * 128], wv[:, k, :], k == 0, k == 7, [wr, "xn"], [pr])
                VCOPY(vtok[:, 4 + t, :], pb[:, 0:512], [pr], ["vtok"])
                if s == NSEG - 1:
                    ACOPY(kout[:], pb[:, 0:512], [pr], [koutr])
                    P.dma("sp", o_v_p[l, t * 128:(t + 1) * 128, :], kout[:], [koutr], ())
            if s == NSEG - 1:
                pb, pr = bank()
                for k in range(8):
                    MM(pb[0:NS, 0:512], xn[:, k, SEG:SEG + NS], wv[:, k, :], k == 0, k == 7, [wr, "xn"], [pr])
                VCOPY(vts[0:NS, 4, :], pb[0:NS, 0:512], [pr], [vtsr])
                ACOPY(kout[0:NS, :], pb[0:NS, 0:512], [pr], [koutr])
                P.dma("sp", o_v_s[l], kout[0:NS, :], [koutr], ())

            def attend(it, nq, nk, qsl, q_t, q_r, k_t, k_r, kc0, vblocks, bias_t, bias_r, extra, out_cols):
                for h in range(8):
                    c, e = h // 2, h % 2
                    i2 = (it * 8 + h) % 2
                    pa, par = pbs[2 * i2], "pb%d" % (2 * i2)
                    pa2, pa2r = pbs[2 * i2 + 1], "pb%d" % (2 * i2 + 1)
                    n1 = min(512, nk)
                    MM(pa[0:nq, 0:n1], q_t[e * 64:(e + 1) * 64, c, qsl], k_t[e * 64:(e + 1) * 64, c, kc0:kc0 + n1],
                       True, True, [q_r, k_r], [par])
                    if nk > 512:
                        MM(pa2[0:nq, 0:nk - 512], q_t[e * 64:(e + 1) * 64, c, qsl],
                           k_t[e * 64:(e + 1) * 64, c, kc0 + 512:kc0 + nk], True, True, [q_r, k_r], [pa2r])
                    VSTT(s1[0:nq, 0:n1], pa[0:nq, 0:n1], 0.125, bias_t[0:nq, h, 0:n1], ALU.mult, ALU.add,
                         [par, bias_r], [s1r])
                    if nk > 512:
                        VSTT(s1[0:nq, 512:nk], pa2[0:nq, 0:nk - 512], 0.125, bias_t[0:nq, h, 512:nk], ALU.mult,
                             ALU.add, [pa2r, bias_r], [s1r])
                    if extra is not None:
                        VTT(s1[0:nq, 0:nk], s1[0:nq, 0:nk], extra[0], ALU.add, [s1r, extra[1]], [s1r])
                    P.op("dve", lambda e_, nq=nq, nk=nk: e_.tensor_reduce(stat[0:nq, 0:1], s1[0:nq, 0:nk], AX.X,
                                                                         ALU.max, negate=True),
                         [s1r], [statr])
                    VMEMSET(stat[0:nq, 1:2], 0.0, [statr])
                    ACT(Pb[0:nq, 0:nk], s1[0:nq, 0:nk], AF.Exp, [s1r, statr], [Pbr, statr],
                        bias=stat[0:nq, 0:1], scale=1.0, accum_out=stat[0:nq, 1:2])
                    ptb = pbs[6][:, :].bitcast(BF16)
                    k0 = 0
                    for bi, (v_ap, v_r, kb) in enumerate(vblocks):
                        TR(ptb[0:kb, bi * 128:bi * 128 + nq], Pb[0:nq, k0:k0 + kb], identB[0:nq, 0:nq],
                           [Pbr, "identB"], ["pb6"])
                        k0 += kb
                    nfull = sum(1 for vb in vblocks if vb[2] == 128)
                    ACOPY(PT[:, 0:nfull, 0:nq], ptb[:, 0:nfull * 128].rearrange("p (a b) -> p a b", b=128)[:, :, 0:nq],
                          ["pb6"], [PTr])
                    for bi, (v_ap, v_r, kb) in enumerate(vblocks):
                        if kb != 128:
                            ACOPY(PT[0:kb, bi, 0:nq], ptb[0:kb, bi * 128:bi * 128 + nq], ["pb6"], [PTr])
                    po, por = pbs[4], "pb4"
                    oc = i2 * 64
                    for bi, (v_ap, v_r, kb) in enumerate(vblocks):
                        MM(po[0:nq, oc:oc + 64], PT[0:kb, bi, 0:nq], v_ap[0:kb, h * 64:(h + 1) * 64],
                           bi == 0, bi == len(vblocks) - 1, [PTr, v_r], [por])
                    P.op("dve", lambda e_, nq=nq: e_.reciprocal(stat[0:nq, 2:3], stat[0:nq, 1:2]), [statr], [statr])
                    VTS(yct[0:nq, h * 64:(h + 1) * 64], po[0:nq, oc:oc + 64], stat[0:nq, 2:3], None, ALU.mult, None,
                        [por, statr], [yctr])
                pyb = pbs[7][:, :].bitcast(BF16)
                for m in range(4):
                    TR(pyb[:, m * 128:m * 128 + nq], yct[0:nq, m * 128:(m + 1) * 128], identB[0:nq, 0:nq],
                       [yctr, "identB"], ["pb7"])
                VCOPY(ycT[:, :, out_cols], pyb[:, 0:512].rearrange("p (a b) -> p a b", b=128)[:, :, 0:nq],
                      ["pb7"], ["ycT"])

            for pp in range(4):
                extra = (km[:, pp, :], kmr) if s == 0 else None
                vbl = [(vtok[:, pp + b, :], "vtok", 128) for b in range(5)]
                attend(pp, 128, 640, slice(pp * 128, (pp + 1) * 128), qT, qTr, kT, "kT", pp * 128, vbl,
                       B2, B2r, extra, slice(pp * 128, (pp + 1) * 128))
            if s == NSEG - 1:
                P.dma("pool", ckb[:], ck[l].rearrange("(t p) f -> p t f", p=128), (), [ckbr])
                P.dma("pool", vts[:, 0:4, :], cv[l].rearrange("(t p) f -> p t f", p=128), (), [vtsr])
                pkb = pbs[7][:, :].bitcast(BF16)
                for m in range(4):
                    for t in range(4):
                        TR(pkb[:, t * 128:(t + 1) * 128], ckb[:, t, m * 128:(m + 1) * 128], identB[:],
                           [ckbr, "identB"], ["pb7"])
                    VCOPY(kTs[:, m, 0:512], pkb[:, 0:512], ["pb7"], [kTsr])
                P.dma("sp", B2[0:NS, :, 0:544], biass[l], (), [B2r])
                vbl = [(vts[:, b, :], vtsr, 128) for b in range(4)] + [(vts[:, 4, :], vtsr, NS)]
                attend(4, NS, 544, slice(SEG, SEG + NS), qT, qTr, kTs, kTsr, 0, vbl, B2, B2r, None,
                       slice(SEG, SEG + NS))
            VCOPY(kT[:, :, 0:512], kT[:, :, 512:1024], ["kT"], ["kT"])
            ACOPY(vtok[:, 0:4, :], vtok[:, 4:8, :], ["vtok"], ["vtok"])

        def pool_segment(l, s):
            AR.reset()
            P.barrier()
            blocks = seg_blocks(s)
            sA, sAr = AR.alloc("sA", [128, 16 + SEG], F32)
            sB, sBr = AR.alloc("sB", [128, 16 + SEG], F32)
            pbf, pbfr = AR.alloc("pbf", [128, 4, LW], BF16)
            tmp16, tmp16r = AR.alloc("tmp16", [128, 16], F32)
            if s == NSEG - 1:
                uas, uasr = AR.alloc("uas", [128, 4, 16 + NS], F32)
                hist, histr = AR.alloc("hist", [15, 512], F32)
                pout, poutr = AR.alloc("pout", [15, 512], F32)
                pouts, poutsr = AR.alloc("pouts", [15, 512], F32)
                P.dma("sp", hist[:], cpool[l], (), [histr])
                for g in range(4):
                    pb, pr = bank()
                    TR(pb[:, 0:15], hist[0:15, g * 128:(g + 1) * 128], identF[0:15, 0:15], [histr, "identF"], [pr])
                    VCOPY(uas[:, g, 1:16], pb[:, 0:15], [pr], [uasr])
            wv, wr = wload(w_in[l], 0, 8, OFF["ua"], 512)
            for g in range(4):
                for (xc0, n, lc0) in blocks:
                    pb, pr = bank()
                    for k in range(8):
                        MM(pb[:, 0:n], wv[:, k, g * 128:(g + 1) * 128], xn[:, k, lc0:lc0 + n], k == 0, k == 7,
                           [wr, "xn"], [pr])
                    if n == SEG:
                        ACOPY(ua[:, g, 16:16 + SEG], pb[:, 0:n], [pr], ["ua"])
                    else:
                        ACOPY(uas[:, g, 16:16 + NS], pb[:, 0:n], [pr], [uasr])
            for (xc0, n, lc0) in blocks:
                ut, utr = (ua, "ua") if n == SEG else (uas, uasr)
                W_ = 16 + n
                for g in range(4):
                    w = 2 << g
                    src, srcr = ut[:, g, 0:W_], utr
                    d = 1
                    bufs = [(sA, sAr), (sB, sBr)]
                    bi = 0
                    while d < w:
                        dst, dstr = bufs[bi]
                        bi ^= 1
                        VTT(dst[:, d:W_], src[:, d:W_], src[:, 0:W_ - d], ALU.add, [srcr], [dstr])
                        src, srcr = dst[:, 0:W_], dstr
                        d *= 2
                    oth, othr = bufs[bi]
                    VTS(oth[:, 16:W_], src[:, 16:W_], 1.0 / w, None, ALU.mult, None, [srcr], [othr])
                    VTT(pbf[:, g, lc0:lc0 + n], oth[:, 16:W_], ut[:, g, 16:W_], ALU.subtract, [othr, utr], [pbfr])
                    if s == 0 and n == SEG:
                        VTT(tmp16[:], src[:, 16:32], c_rc[:, g, :], ALU.mult, [srcr, "c_rc"], [tmp16r])
                        VTT(pbf[:, g, 0:16], tmp16[:], ut[:, g, 16:32], ALU.subtract, [tmp16r, utr], [pbfr])
                    pb, pr = bank()
                    MM(pb[:, 0:n], c_pmap[:, l, g, :], pbf[:, g, lc0:lc0 + n], True, True, ["c_pmap", pbfr], [pr])
                    VTS(yaT[:, g, lc0:lc0 + n], pb[:, 0:n], c_ps[:, l, g:g + 1], None, ALU.mult, None,
                        [pr, "c_ps"], ["yaT"])
            if s == NSEG - 1:
                for (ut, utr, W_, ot, otr, dst) in ((ua, "ua", 16 + SEG, pout, poutr, o_pool_p[l]),
                                                    (uas, uasr, 16 + NS, pouts, poutsr, o_pool_s[l])):
                    pb, pr = bank()
                    for g in range(4):
                        TR(pb[0:15, g * 128:(g + 1) * 128], ut[:, g, W_ - 15:W_], identF[:], [utr, "identF"], [pr])
                    VCOPY(ot[:], pb[0:15, 0:512], [pr], [otr])
                    P.dma("sp", dst, ot[:], [otr], ())
            VCOPY(ua[:, :, 1:16], ua[:, :, 16 + SEG - 15:16 + SEG], ["ua"], ["ua"])

        def merge_segment(l, s):
            AR.reset()
            P.barrier()
            blocks = seg_blocks(s)
            acc, accr = AR.alloc("acc", [128, 4, LW], F32)
            sig, sigr = AR.alloc("sig", [128, LW], F32)
            tmp, tmpr = AR.alloc("tmp", [128, LW], F32)
            mg, mgr = AR.alloc("mg", [128, 8, LW], BF16)
            ys = ((yaT, "yaT"), (ybT, "ybT"), (ycT, "ycT"))
            for mgi in range(2):
                for g in range(3):
                    wg, wgr = wload(w_in[l], 0, 8, OFF["gate"] + g * D + mgi * 512, 512)
                    wb, wbr = wload(w_br[l, g], 0, 4, mgi * 512, 512)
                    yt, ytr = ys[g]
                    for mi in range(4):
                        for (xc0, n, lc0) in blocks:
                            pg, pgr = bank()
                            for k in range(8):
                                MM(pg[:, 0:n], wg[:, k, mi * 128:(mi + 1) * 128], xn[:, k, lc0:lc0 + n], k == 0,
                                   k == 7, [wgr, "xn"], [pgr])
                            pbb, pbr = bank()
                            for k in range(4):
                                MM(pbb[:, 0:n], wb[:, k, mi * 128:(mi + 1) * 128], yt[:, k, lc0:lc0 + n], k == 0,
                                   k == 3, [wbr, ytr], [pbr])
                            ACT(sig[:, lc0:lc0 + n], pg[:, 0:n], AF.Sigmoid, [pgr], [sigr])
                            if g == 0:
                                VTT(acc[:, mi, lc0:lc0 + n], sig[:, lc0:lc0 + n], pbb[:, 0:n], ALU.mult,
                                    [sigr, pbr], [accr])
                            else:
                                VTT(tmp[:, lc0:lc0 + n], sig[:, lc0:lc0 + n], pbb[:, 0:n], ALU.mult,
                                    [sigr, pbr], [tmpr])
                                if g == 1:
                                    VTT(acc[:, mi, lc0:lc0 + n], acc[:, mi, lc0:lc0 + n], tmp[:, lc0:lc0 + n],
                                        ALU.add, [accr, tmpr], [accr])
                                else:
                                    VTT(mg[:, mgi * 4 + mi, lc0:lc0 + n], acc[:, mi, lc0:lc0 + n],
                                        tmp[:, lc0:lc0 + n], ALU.add, [accr, tmpr], [mgr])
            for half in range(2):
                wv, wr = wload(w_out[l], 0, 8, half * 512, 512)
                for mi in range(4):
                    m = half * 4 + mi
                    for (xc0, n, lc0) in blocks:
                        pb, pr = bank()
                        for k in range(8):
                            MM(pb[:, 0:n], wv[:, k, mi * 128:(mi + 1) * 128], mg[:, k, lc0:lc0 + n], k == 0, k == 7,
                               [wr, mgr], [pr])
                        VTT(xT[:, m, xc0:xc0 + n], xT[:, m, xc0:xc0 + n], pb[:, 0:n], ALU.add, ["xT", pr], ["xT"])

        def ffn_segment(l, s):
            AR.reset()
            P.barrier()
            blocks = seg_blocks(s)
            sq, sqr = AR.alloc("sq", [128, 8, LW], BF16)
            hT, hTr = AR.alloc("hT", [128, 22, LW], BF16)
            ha, har = AR.alloc("ha", [128, LW], BF16)
            rmsnorm(blocks, c_fng[:, l, :], xn, "xn", sq, sqr)
            for sl in range(6):
                nc_ = 512 if sl < 5 else 256
                wa, war = wload(w_f1[l], 0, 8, sl * 512, nc_)
                wgt, wgtr = wload(w_f1[l], 0, 8, DFF + sl * 512, nc_)
                for mi in range(nc_ // 128):
                    j = sl * 4 + mi
                    for (xc0, n, lc0) in blocks:
                        pa, par = bank()
                        for k in range(8):
                            MM(pa[:, 0:n], wa[:, k, mi * 128:(mi + 1) * 128], xn[:, k, lc0:lc0 + n], k == 0, k == 7,
                               [war, "xn"], [par])
                        pg, pgr = bank()
                        for k in range(8):
                            MM(pg[:, 0:n], wgt[:, k, mi * 128:(mi + 1) * 128], xn[:, k, lc0:lc0 + n], k == 0, k == 7,
                               [wgtr, "xn"], [pgr])
                        ACT(ha[:, lc0:lc0 + n], pa[:, 0:n], AF.Silu, [par], [har])
                        VTT(hT[:, j, lc0:lc0 + n], ha[:, lc0:lc0 + n], pg[:, 0:n], ALU.mult, [har, pgr], [hTr])
            for m in range(8):
                wv, wr = wload(w_f2[l], 0, 22, m * 128, 128)
                for (xc0, n, lc0) in blocks:
                    pb, pr = bank()
                    for k in range(22):
                        MM(pb[:, 0:n], wv[:, k, :], hT[:, k, lc0:lc0 + n], k == 0, k == 21, [wr, hTr], [pr])
                    VTT(xT[:, m, xc0:xc0 + n], xT[:, m, xc0:xc0 + n], pb[:, 0:n], ALU.add, ["xT", pr], ["xT"])

        def halo_layer0():
            AR.reset()
            P.barrier()
            xhf, xhfr = AR.alloc("xhf", [128, 8, 512], F32)
            xh, xhr = AR.alloc("xh", [128, 8, 512], BF16)
            sq, sqr = AR.alloc("sq", [128, 8, LW], BF16)
            xtok, xr = AR.alloc("xtok", [128, 4, D], F32)
            P.dma("sp", xtok[:], xin[0:512, :].rearrange("(t p) f -> p t f", p=128), (), [xr])
            for k in range(8):
                pb, pr = bank()
                for t in range(4):
                    TR(pb[:, t * 128:(t + 1) * 128], xtok[:, t, k * 128:(k + 1) * 128], identF[:], [xr, "identF"], [pr])
                VCOPY(xhf[:, k, :], pb[:, 0:512], [pr], [xhfr])
            rmsnorm([(0, 512, 0)], c_ang[:, 0, :], xh, xhr, sq, sqr, xsrc=xhf, xsrc_res=xhfr)
            halo_proj(0, xh, xhr)

        def send_halo_next(l):
            AR.reset()
            P.barrier()
            sq, sqr = AR.alloc("sq", [128, 8, LW], BF16)
            o0 = AR.off
            xsf, xsr = AR.alloc("xsend", [128, 8, 256], F32)
            xs_ = arena_t[:, o0 // 2:o0 // 2 + 8 * 512].rearrange("p (a b) -> p a b", a=8)
            rmsnorm([(3 * SEG, SEG, 0)], c_ang[:, l + 1, :], xs_, xsr, sq, sqr)
            for g in range(4):
                P.dma("sp", cc1_in[g].ap().rearrange("(k p) t -> p k t", p=128), xsf[:, 2 * g:2 * g + 2, :],
                      [xsr], ["cc1_in%d" % g])

                def ccf(e, g=g):
                    return e.collective_compute("AllGather", ALU.bypass, replica_groups=[list(range(NCORES))],
                                                ins=[cc1_in[g].ap().opt()], outs=[cc1_out[g].ap().opt()])
                P.op("pool", ccf, ["cc1_in%d" % g], ["cc1_out%d" % g])

        def halo_layer1():
            AR.reset()
            P.barrier()
            o0 = AR.off
            xhf, xhr = AR.alloc("xh", [128, 8, 256], F32)
            xh = arena_t[:, o0 // 2:o0 // 2 + 8 * 512].rearrange("p (a b) -> p a b", a=8)
            P.op("pool", lambda e: e.memset(xhf[:], 0.0), (), [xhr])
            for k in range(8):
                P.dma_fn("pool", lambda e, k=k: e.indirect_dma_start(
                    out=xhf[:, k, :], out_offset=None, in_=cc1_out[k // 2].ap(),
                    in_offset=bass.IndirectOffsetOnAxis(ap=c_gidx[:, k:k + 1], axis=0),
                    bounds_check=NCORES * 256 - 1, oob_is_err=False),
                    ["cc1_out%d" % (k // 2), "c_gidx", xhr], [xhr])
            halo_proj(1, xh, xhr)

        def final_out(s):
            AR.reset()
            P.barrier()
            blocks = seg_blocks(s)
            sq, sqr = AR.alloc("sq", [128, 8, LW], BF16)
            yT, yTr = AR.alloc("yT", [128, 8, LW], F32)
            ytok, ytokr = AR.alloc("ytok", [128, D], F32)
            rmsnorm(blocks, c_fing[:, :], None, yTr, sq, sqr, f32out=yT)
            for (xc0, n, lc0) in blocks:
                for t in range((n + 127) // 128):
                    r = min(128, n - t * 128)
                    for kh in range(2):
                        pb, pr = bank()
                        for kk in range(4):
                            k = kh * 4 + kk
                            TR(pb[0:r, kk * 128:(kk + 1) * 128], yT[:, k, lc0 + t * 128:lc0 + t * 128 + r], identF[:],
                               [yTr, "identF"], [pr])
                        (VCOPY if kh == 0 else ACOPY)(ytok[0:r, kh * 512:(kh + 1) * 512], pb[0:r, 0:512], [pr], [ytokr])
                    if n == SEG:
                        P.dma("sp", y_out[xc0 + t * 128:xc0 + t * 128 + r, :], ytok[0:r, :], [ytokr], ())
                    else:
                        P.dma("sp", ys_out, ytok[0:r, :], [ytokr], ())

        stg = [0]

        def stage():
            stg[0] += 1
            if stg[0] > KSTOP[0]:
                raise _Stop()
        try:
            for l in range(NL):
                VMEMSET(Sst[:], 0.0, ["Sst"])
                VMEMSET(Sb[:], 0.0, ["Sb"])
                VMEMSET(Dacc[:], 1.0, ["Dacc"])
                for s in range(NSEG):
                    stage()
                    AR.reset()
                    P.barrier()
                    sq, sqr = AR.alloc("sq", [128, 8, LW], BF16)
                    rmsnorm(seg_blocks(s, False), c_ang[:, l, :], xn, "xn", sq, sqr)
                    gla_segment(l, s, compute_o=False)
                stage()
                gla_exchange(l)
                VCOPY(Sst[:], Strue[:], ["Strue"], ["Sst"])
                VCOPY(Sb[:], Strue[:], ["Strue"], ["Sb"])
                load_state_sample(l)
                stage()
                if l == 0:
                    halo_layer0()
                else:
                    halo_layer1()
                for s in range(NSEG):
                    stage()
                    attention_segment(l, s)
                    stage()
                    gla_segment(l, s, compute_o=True)
                    stage()
                    pool_segment(l, s)
                    stage()
                    merge_segment(l, s)
                    stage()
                    ffn_segment(l, s)
                store_state(Sst, "Sst", o_gla_p[l])
                store_state(Ssm, "Ssm", o_gla_s[l])
                if l + 1 < NL and not _os.environ.get("NOSEND"):
                    send_halo_next(l)
            for s in range(NSEG):
                stage()
                final_out(s)

        except _Stop:
            pass
        with nc.Block() as block:
            P.replay(block)
    return nc


_CACHE = {}
KSTOP = [10 ** 9]
import os as _os
GSTOP = int(_os.environ.get('GSTOP', '99'))


class _Stop(Exception):
    pass


def _rel_tables(rel_bias):
    L = rel_bias.shape[0]
    q = np.arange(128)
    e = q // 64
    qi = q % 64
    j = np.arange(640)
    k_rel = (j[None, :] - 512) - 64 * e[:, None]
    valid = (k_rel >= -512) & (k_rel < 64)
    idx = np.clip(qi[:, None] - k_rel, -128, 128) + 128
    b2 = rel_bias[:, :, idx]
    b2 = np.where(valid[None, None], b2, np.float32(NEG)).astype(np.float32)
    b2 = np.ascontiguousarray(b2.transpose(0, 2, 1, 3))
    qs = np.arange(NS)
    js = np.arange(544)
    idxs = np.clip(qs[:, None] - (js[None, :] - 512), -128, 128) + 128
    bs = np.ascontiguousarray(rel_bias[:, :, idxs].transpose(0, 2, 1, 3)).astype(np.float32)
    return b2, bs


def kernel(x_prompt, x_sample, cache_pool, state_gla, cache_k, cache_v, attn_norm_g, w_in, w_gate2, b_gate,
           gla_norm_g, pool_map, pool_scale, rel_bias, w_branch, w_out, ffn_norm_g, w_ffn_in, w_ffn_out,
           final_norm_g):
    f = lambda a: np.ascontiguousarray(np.asarray(a, dtype=np.float32))
    x_prompt, x_sample = f(x_prompt), f(x_sample)
    import os
    if os.environ.get("KSTOP"):
        KSTOP[0] = int(os.environ["KSTOP"])
    if "nc" not in _CACHE:
        _CACHE["nc"] = build()
    nc = _CACHE["nc"]

    def pk(v, k):
        v = f(v)
        return np.ascontiguousarray(v.reshape(v.shape[:-1] + (k, 128)).swapaxes(-1, -2))

    sw = lambda a: np.ascontiguousarray(np.swapaxes(a, 0, 1))
    b2, bs = _rel_tables(f(rel_bias))
    shared = {
        "ang": sw(pk(attn_norm_g, 8)), "fng": sw(pk(ffn_norm_g, 8)), "fing": pk(final_norm_g, 8),
        "w_in": f(w_in), "wg2": sw(f(w_gate2)), "bgate": sw(pk(b_gate, 2)),
        "gnorm": sw(f(gla_norm_g).reshape(NL, 128, 1)),
        "pmap": np.ascontiguousarray(f(pool_map).transpose(2, 0, 1, 3)),
        "pscale": sw(pk(pool_scale, 4)), "w_br": f(w_branch), "w_out": f(w_out), "w_f1": f(w_ffn_in),
        "w_f2": f(w_ffn_out), "bias2": b2, "biass": bs,
    }
    xp = x_prompt[0]
    in_maps = []
    for c in range(NCORES):
        xin = np.zeros((2560, D), np.float32)
        if c > 0:
            xin[0:512] = xp[c * T - 512:c * T]
        xin[512:] = xp[c * T:(c + 1) * T]
        km = np.zeros((128, 4, 640), np.float32)
        rc = np.zeros((128, 4, 16), np.float32)
        for g in range(4):
            w = 2 << g
            if c == 0:
                rc[:, g, :] = 1.0 / np.minimum(w, np.arange(16) + 1.0)
            else:
                rc[:, g, :] = 1.0 / w
        if c == 0:
            for pp in range(4):
                km[:, pp, 0:512 - 128 * pp] = NEG
        mvec = np.zeros((128, 8), np.float32)
        mvec[:, :c] = 1.0
        if c > 0:
            gidx = ((c - 1) * 256 + (np.arange(8)[None, :] % 2) * 128 + np.arange(128)[:, None]).astype(np.int32)
        else:
            gidx = np.full((128, 8), NCORES * 256 + 7, np.int32)
        m = dict(shared)
        m.update({
            "xin": xin, "xs": f(x_sample[c]), "cpool": f(cache_pool[:, c]), "sgla": f(state_gla[:, c]),
            "ck": f(cache_k[:, c]).reshape(NL, 512, 512), "cv": f(cache_v[:, c]).reshape(NL, 512, 512),
            "kmask": km, "rc": rc, "mvec": mvec, "gidx": gidx,
        })
        in_maps.append(m)
    res = run_bass_kernel_spmd(nc, in_maps, core_ids=list(range(NCORES))).results
    y_prompt = np.concatenate([res[c]["y"] for c in range(NCORES)], axis=0)[None]
    y_sample = np.stack([res[c]["ys"] for c in range(NCORES)], axis=0)
    last = NCORES - 1
    new_pool_prompt = res[last]["o_pool_p"][:, None]
    new_pool_sample = np.stack([res[c]["o_pool_s"] for c in range(NCORES)], axis=1)
    new_gla_prompt = res[last]["o_gla_p"][:, None]
    new_gla_sample = np.stack([res[c]["o_gla_s"] for c in range(NCORES)], axis=1)
    new_k_prompt = res[last]["o_k_p"].reshape(NL, 1, 512, 8, 64)
    new_k_sample = np.stack([res[c]["o_k_s"] for c in range(NCORES)], axis=1).reshape(NL, NCORES, NS, 8, 64)
    new_v_prompt = res[last]["o_v_p"].reshape(NL, 1, 512, 8, 64)
    new_v_sample = np.stack([res[c]["o_v_s"] for c in range(NCORES)], axis=1).reshape(NL, NCORES, NS, 8, 64)
    outs = (y_prompt, y_sample, new_pool_prompt, new_pool_sample, new_gla_prompt, new_gla_sample,
            new_k_prompt, new_k_sample, new_v_prompt, new_v_sample)
    return tuple(np.ascontiguousarray(o, dtype=np.float32) for o in outs)
```
